# Optimizing a Trainium2 kernel written in Bass

```python
import math
import jax, jax.numpy as jnp
from jax import lax
import numpy as np

D_MODEL = 2048
BATCH = 4
SEQ = 4096
DEPTH = 4

N_MIXERS = 3
EPS = 1e-6
POOL_WINDOWS = (2, 4, 8, 16)
N_POOL_GROUPS = len(POOL_WINDOWS)
POOL_GROUP = D_MODEL // N_POOL_GROUPS
SWA_HEADS = 32
SWA_KV_HEADS = 4
SWA_GROUP = SWA_HEADS // SWA_KV_HEADS
SWA_HEAD_DIM = D_MODEL // SWA_HEADS
SWA_WINDOW = 128
BLOCK = 128
MLA_HEADS = 16
MLA_NOPE = 128
MLA_ROPE = 64
MLA_V = 128
MLA_Q_RANK = 512
MLA_KV_RANK = 512
ROPE_THETA = 10000.0
D_FF = 5632
CONV_W = 3

kernel_name = "hybrid_pool_swa_mla_convglu_encoder"


def rmsnorm(x, g):
    xf = x.astype(jnp.float32)
    y = xf * lax.rsqrt(jnp.mean(xf * xf, axis=-1, keepdims=True) + EPS)
    return (y * g.astype(jnp.float32)).astype(x.dtype)


def alibi_slopes(n):
    return jnp.asarray(2.0 ** (-8.0 * np.arange(1, n + 1) / n), dtype=jnp.float32)


def rope_tables(positions, dim):
    inv = ROPE_THETA ** (-jnp.arange(0, dim, 2, dtype=jnp.float32) / dim)
    ang = positions.astype(jnp.float32)[:, None] * inv[None, :]
    return jnp.cos(ang), jnp.sin(ang)


def apply_rope(x, cos, sin):
    x1, x2 = jnp.split(x.astype(jnp.float32), 2, axis=-1)
    return jnp.concatenate([x1 * cos - x2 * sin, x2 * cos + x1 * sin], axis=-1).astype(x.dtype)


def pool_mixer(h, w_groups, scale):
    B, S, D = h.shape
    hf = h.astype(jnp.float32).reshape(B, S, N_POOL_GROUPS, POOL_GROUP)
    csum = jnp.concatenate(
        [jnp.zeros((B, 1, N_POOL_GROUPS, POOL_GROUP), jnp.float32), jnp.cumsum(hf, axis=1)],
        axis=1)
    left = np.array([w // 2 for w in POOL_WINDOWS], dtype=np.int32)
    right = np.array([w - 1 - w // 2 for w in POOL_WINDOWS], dtype=np.int32)
    t = jnp.arange(S, dtype=jnp.int32)[:, None]
    hi = jnp.clip(t + right[None, :] + 1, 0, S)
    lo = jnp.clip(t - left[None, :], 0, S)
    g_idx = jnp.arange(N_POOL_GROUPS)[None, :]
    win_sum = csum[:, hi, g_idx] - csum[:, lo, g_idx]
    count = (hi - lo).astype(jnp.float32)[None, :, :, None]
    pooled = (win_sum / count - hf).astype(h.dtype)
    y = jnp.einsum('bsgc,gcd->bsgd', pooled, w_groups).reshape(B, S, D)
    return y * scale


def swa_mixer(h, positions, w_qkv, q_gain, k_gain, sinks, w_o):
    B, S, D = h.shape
    nq = SWA_HEADS * SWA_HEAD_DIM
    nkv = SWA_KV_HEADS * SWA_HEAD_DIM
    qkv = h @ w_qkv
    q = qkv[..., :nq].reshape(B, S, SWA_KV_HEADS, SWA_GROUP, SWA_HEAD_DIM)
    k = qkv[..., nq:nq + nkv].reshape(B, S, SWA_KV_HEADS, SWA_HEAD_DIM)
    v = qkv[..., nq + nkv:].reshape(B, S, SWA_KV_HEADS, SWA_HEAD_DIM)
    q = rmsnorm(q, q_gain) * (SWA_HEAD_DIM ** -0.5)
    k = rmsnorm(k, k_gain)
    pad = SWA_WINDOW
    span = BLOCK + 2 * SWA_WINDOW
    k_pad = jnp.pad(k, ((0, 0), (pad, pad), (0, 0), (0, 0)))
    v_pad = jnp.pad(v, ((0, 0), (pad, pad), (0, 0), (0, 0)))
    pos_pad = jnp.pad(positions, (pad, pad))
    valid_pad = jnp.pad(jnp.ones((S,), dtype=bool), (pad, pad))
    slopes = alibi_slopes(SWA_HEADS).reshape(SWA_KV_HEADS, SWA_GROUP)
    sink = sinks.astype(jnp.float32).reshape(SWA_KV_HEADS, SWA_GROUP)[None, :, :, None]

    def block(j):
        start = j * BLOCK
        qb = lax.dynamic_slice_in_dim(q, start, BLOCK, axis=1)
        kb = lax.dynamic_slice_in_dim(k_pad, start, span, axis=1)
        vb = lax.dynamic_slice_in_dim(v_pad, start, span, axis=1)
        pq = lax.dynamic_slice_in_dim(positions, start, BLOCK)
        pk = lax.dynamic_slice_in_dim(pos_pad, start, span)
        ok = lax.dynamic_slice_in_dim(valid_pad, start, span)
        qi = start + jnp.arange(BLOCK)
        ki = start - SWA_WINDOW + jnp.arange(span)
        in_win = (jnp.abs(qi[:, None] - ki[None, :]) <= SWA_WINDOW) & ok[None, :]
        s = jnp.einsum('bqkgd,bskd->bkgqs', qb, kb).astype(jnp.float32)
        dist = jnp.abs(pq[:, None] - pk[None, :]).astype(jnp.float32)
        s = s - slopes[:, :, None, None] * dist
        s = jnp.where(in_win, s, -jnp.inf)
        m = jnp.maximum(jnp.max(s, axis=-1), sink)
        p = jnp.exp(s - m[..., None])
        denom = jnp.sum(p, axis=-1) + jnp.exp(sink - m)
        p = (p / denom[..., None]).astype(vb.dtype)
        return jnp.einsum('bkgqs,bskd->bqkgd', p, vb)

    o = lax.map(block, jnp.arange(S // BLOCK))
    o = jnp.transpose(o, (1, 0, 2, 3, 4, 5)).reshape(B, S, nq)
    return o @ w_o


def mla_mixer(h, positions, w_down, q_a_gain, kv_a_gain, w_uq, w_ukv,
              qn_gain, qr_gain, kn_gain, kr_gain, w_o):
    B, S, D = h.shape
    d = h @ w_down
    cq = rmsnorm(d[..., :MLA_Q_RANK], q_a_gain)
    ckv = rmsnorm(d[..., MLA_Q_RANK:MLA_Q_RANK + MLA_KV_RANK], kv_a_gain)
    k_pe = d[..., MLA_Q_RANK + MLA_KV_RANK:]
    q = (cq @ w_uq).reshape(B, S, MLA_HEADS, MLA_NOPE + MLA_ROPE)
    kv = (ckv @ w_ukv).reshape(B, S, MLA_HEADS, MLA_NOPE + MLA_V)
    q_nope = rmsnorm(q[..., :MLA_NOPE], qn_gain)
    q_pe = rmsnorm(q[..., MLA_NOPE:], qr_gain)
    k_nope = rmsnorm(kv[..., :MLA_NOPE], kn_gain)
    v = kv[..., MLA_NOPE:]
    k_pe = rmsnorm(k_pe, kr_gain)
    cos, sin = rope_tables(positions, MLA_ROPE)
    q_pe = apply_rope(q_pe, cos[:, None, :], sin[:, None, :])
    k_pe = apply_rope(k_pe, cos, sin)
    scale = (MLA_NOPE + MLA_ROPE) ** -0.5
    q_nope = q_nope * scale
    q_pe = q_pe * scale

    def block(j):
        start = j * BLOCK
        qn = lax.dynamic_slice_in_dim(q_nope, start, BLOCK, axis=1)
        qp = lax.dynamic_slice_in_dim(q_pe, start, BLOCK, axis=1)
        s = (jnp.einsum('bqhd,bshd->bhqs', qn, k_nope).astype(jnp.float32)
             + jnp.einsum('bqhr,bsr->bhqs', qp, k_pe).astype(jnp.float32))
        p = jax.nn.softmax(s, axis=-1).astype(v.dtype)
        return jnp.einsum('bhqs,bshd->bqhd', p, v)

    o = lax.map(block, jnp.arange(S // BLOCK))
    o = jnp.transpose(o, (1, 0, 2, 3, 4)).reshape(B, S, MLA_HEADS * MLA_V)
    return o @ w_o


def conv_glu(h, w_in, conv_w, conv_b, w_out):
    u = h @ w_in
    g, val = u[..., :D_FF], u[..., D_FF:]
    gp = jnp.pad(g, ((0, 0), (1, 1), (0, 0)))
    g = gp[:, :-2] * conv_w[0] + gp[:, 1:-1] * conv_w[1] + gp[:, 2:] * conv_w[2] + conv_b
    return (jax.nn.silu(g) * val) @ w_out


def setup_inputs(seed: int = 0) -> dict:
    key = jax.random.key(seed)
    ks = iter(jax.random.split(key, 40))
    f32 = jnp.float32
    n_pool = (DEPTH + 2) // 3
    n_swa = (DEPTH + 1) // 3
    n_mla = DEPTH // 3
    res = (2.0 * DEPTH) ** -0.5

    def w(shape, fan_in, gain=1.0):
        return jax.random.normal(next(ks), shape, f32) * (gain * fan_in ** -0.5)

    def gain(shape):
        return 1.0 + 0.02 * jax.random.normal(next(ks), shape, f32)

    x = jax.random.normal(next(ks), (BATCH, SEQ, D_MODEL), f32)
    positions = jnp.arange(SEQ, dtype=jnp.int32)
    norm_mix_g = gain((DEPTH, D_MODEL))
    norm_ffn_g = gain((DEPTH, D_MODEL))
    pool_w = w((n_pool, N_POOL_GROUPS, POOL_GROUP, POOL_GROUP), POOL_GROUP, res)
    pool_scale = 1.0 + 0.1 * jax.random.normal(next(ks), (n_pool, D_MODEL), f32)
    n_qkv = SWA_HEADS * SWA_HEAD_DIM + 2 * SWA_KV_HEADS * SWA_HEAD_DIM
    swa_w_qkv = w((n_swa, D_MODEL, n_qkv), D_MODEL)
    swa_q_gain = gain((n_swa, SWA_HEAD_DIM))
    swa_k_gain = gain((n_swa, SWA_HEAD_DIM))
    swa_sinks = 0.5 * jax.random.normal(next(ks), (n_swa, SWA_HEADS), f32)
    swa_w_o = w((n_swa, SWA_HEADS * SWA_HEAD_DIM, D_MODEL), SWA_HEADS * SWA_HEAD_DIM, res)
    mla_w_down = w((n_mla, D_MODEL, MLA_Q_RANK + MLA_KV_RANK + MLA_ROPE), D_MODEL)
    mla_q_a_gain = gain((n_mla, MLA_Q_RANK))
    mla_kv_a_gain = gain((n_mla, MLA_KV_RANK))
    mla_w_uq = w((n_mla, MLA_Q_RANK, MLA_HEADS * (MLA_NOPE + MLA_ROPE)), MLA_Q_RANK)
    mla_w_ukv = w((n_mla, MLA_KV_RANK, MLA_HEADS * (MLA_NOPE + MLA_V)), MLA_KV_RANK)
    mla_qn_gain = gain((n_mla, MLA_NOPE))
    mla_qr_gain = gain((n_mla, MLA_ROPE))
    mla_kn_gain = gain((n_mla, MLA_NOPE))
    mla_kr_gain = gain((n_mla, MLA_ROPE))
    mla_w_o = w((n_mla, MLA_HEADS * MLA_V, D_MODEL), MLA_HEADS * MLA_V, res)
    ffn_w_in = w((DEPTH, D_MODEL, 2 * D_FF), D_MODEL)
    ffn_conv_w = w((DEPTH, CONV_W, D_FF), CONV_W)
    ffn_conv_b = 0.02 * jax.random.normal(next(ks), (DEPTH, D_FF), f32)
    ffn_w_out = w((DEPTH, D_FF, D_MODEL), D_FF, res)
    return {
        "x": x, "positions": positions,
        "norm_mix_g": norm_mix_g, "norm_ffn_g": norm_ffn_g,
        "pool_w": pool_w, "pool_scale": pool_scale,
        "swa_w_qkv": swa_w_qkv, "swa_q_gain": swa_q_gain, "swa_k_gain": swa_k_gain,
        "swa_sinks": swa_sinks, "swa_w_o": swa_w_o,
        "mla_w_down": mla_w_down, "mla_q_a_gain": mla_q_a_gain, "mla_kv_a_gain": mla_kv_a_gain,
        "mla_w_uq": mla_w_uq, "mla_w_ukv": mla_w_ukv,
        "mla_qn_gain": mla_qn_gain, "mla_qr_gain": mla_qr_gain,
        "mla_kn_gain": mla_kn_gain, "mla_kr_gain": mla_kr_gain, "mla_w_o": mla_w_o,
        "ffn_w_in": ffn_w_in, "ffn_conv_w": ffn_conv_w, "ffn_conv_b": ffn_conv_b,
        "ffn_w_out": ffn_w_out,
    }


def reference(x, positions, norm_mix_g, norm_ffn_g, pool_w, pool_scale,
              swa_w_qkv, swa_q_gain, swa_k_gain, swa_sinks, swa_w_o,
              mla_w_down, mla_q_a_gain, mla_kv_a_gain, mla_w_uq, mla_w_ukv,
              mla_qn_gain, mla_qr_gain, mla_kn_gain, mla_kr_gain, mla_w_o,
              ffn_w_in, ffn_conv_w, ffn_conv_b, ffn_w_out):
    for i in range(DEPTH):
        kind = i % N_MIXERS
        j = i // N_MIXERS
        h = rmsnorm(x, norm_mix_g[i])
        if kind == 0:
            y = pool_mixer(h, pool_w[j], pool_scale[j])
        elif kind == 1:
            y = swa_mixer(h, positions, swa_w_qkv[j], swa_q_gain[j], swa_k_gain[j],
                          swa_sinks[j], swa_w_o[j])
        else:
            y = mla_mixer(h, positions, mla_w_down[j], mla_q_a_gain[j], mla_kv_a_gain[j],
                          mla_w_uq[j], mla_w_ukv[j], mla_qn_gain[j], mla_qr_gain[j],
                          mla_kn_gain[j], mla_kr_gain[j], mla_w_o[j])
        x = x + y
        h = rmsnorm(x, norm_ffn_g[i])
        x = x + conv_glu(h, ffn_w_in[i], ffn_conv_w[i], ffn_conv_b[i], ffn_w_out[i])
    return x
```

```python
import numpy as np
from contextlib import ExitStack
import concourse.bass as bass
import concourse.mybir as mybir
from concourse.bass_utils import run_bass_kernel_spmd

F32 = mybir.dt.float32
BF16 = mybir.dt.bfloat16
I32 = mybir.dt.int32
ALU = mybir.AluOpType
ACT = mybir.ActivationFunctionType
AX = mybir.AxisListType

D = 2048
DC = 16
S = 4096
NB = 4
DFF = 5632
FC = 44
EPS = 1e-6
NTOK = 2048
TG = 512
NG = NTOK // TG


class Buf:
    __slots__ = ("w", "r", "name", "excl")

    def __init__(self, name=""):
        self.w = None
        self.r = []
        self.name = name
        self.excl = False


class Prog:
    ENG = ("sync", "scalar", "gpsimd", "vector", "tensor")

    def __init__(self, nc):
        self.nc = nc
        self.es = ExitStack()
        self.ops = {e: [] for e in self.ENG}
        self.waited = {e: {} for e in self.ENG}
        self.dsem = {}
        self.needed = {e: set() for e in self.ENG}
        self.nbuf = 0
        self.all_bufs = []

    def sb(self, name, shape, dt):
        return self.es.enter_context(self.nc.sbuf_tensor(name, list(shape), dt))

    def ps(self, name, shape, dt=F32):
        return self.es.enter_context(self.nc.psum_tensor(name, list(shape), dt))

    def buf(self, name=""):
        b = Buf(name)
        self.all_bufs.append(b)
        return b

    def bufs(self, n, name=""):
        return [self.buf(f"{name}{i}") for i in range(n)]

    def barrier(self):
        live = [b for b in self.all_bufs if b.w is not None or b.r]
        for e in ("vector", "scalar", "gpsimd", "tensor", "sync"):
            self.op(e, lambda eng: eng.nop(), writes=live)
        self.all_bufs = list(live)

    def _deps(self, eng, reads, writes):
        deps = []
        for b in reads:
            if b.w is not None:
                deps.append(b.w)
            if b.excl:
                deps.extend(d for d in b.r if d[0] != eng)
        for b in writes:
            if b.w is not None:
                deps.append(b.w)
            deps.extend(b.r)
        out = []
        wt = self.waited[eng]
        for d in deps:
            prod, val = d
            if prod == eng and eng in ("tensor", "sync"):
                continue
            if wt.get(prod, -1) >= val:
                continue
            wt[prod] = val
            out.append(d)
            if prod in self.needed:
                self.needed[prod].add(val)
        return out

    def op(self, eng, fn, reads=(), writes=()):
        deps = self._deps(eng, reads, writes)
        idx = len(self.ops[eng])
        self.ops[eng].append((deps, fn, None))
        me = (eng, idx)
        for b in reads:
            b.r.append(me)
        for b in writes:
            b.w = me
            b.r = []
        return me

    def dma(self, eng, out, in_, reads=(), writes=(), key=None, **kw):
        deps = self._deps(eng, reads, writes)
        if key not in self.dsem:
            self.dsem[key] = [self.es.enter_context(self.nc.semaphore("d_" + key)), 0]
        ent = self.dsem[key]
        ent[1] += 16
        me = ("d:" + key, ent[1])
        self.ops[eng].append((deps, lambda e: e.dma_start(out=out, in_=in_, **kw), (ent[0], 16)))
        for b in reads:
            b.r.append(me)
        for b in writes:
            b.w = me
            b.r = []
        return me

    def finalize(self, final_waits=()):
        nc = self.nc
        esem = {e: self.es.enter_context(nc.semaphore("e_" + e)) for e in self.ENG}
        rank = {}
        for e in self.ENG:
            srt = sorted(self.needed[e])
            rank[e] = {v: i + 1 for i, v in enumerate(srt)}

        def emit(eng_name, engine):
            for idx, (deps, fn, dmainfo) in enumerate(self.ops[eng_name]):
                for prod, val in deps:
                    if prod.startswith("d:"):
                        engine.wait_ge(self.dsem[prod[2:]][0], val)
                    else:
                        engine.wait_ge(esem[prod], rank[prod][val])
                ins = fn(engine)
                if dmainfo is not None:
                    ins.then_inc(dmainfo[0], dmainfo[1])
                elif idx in rank[eng_name]:
                    ins.then_inc(esem[eng_name], 1)
            if eng_name == "sync":
                for key in final_waits:
                    if key not in self.dsem:
                        continue
                    h, c = self.dsem[key]
                    engine.wait_ge(h, c)

        with nc.Block() as block:
            block.sync(lambda e: emit("sync", e))
            block.scalar(lambda e: emit("scalar", e))
            block.gpsimd(lambda e: emit("gpsimd", e))
            block.vector(lambda e: emit("vector", e))
            block.tensor(lambda e: emit("tensor", e))
        self.es.close()


class Ctx:
    pass


def setup_common(p, c):
    nc = p.nc
    c.ident_f = p.sb("ident_f", [128, 128], F32)
    c.ident = p.sb("ident", [128, 128], BF16)
    c.ones = p.sb("ones", [128, 128], BF16)
    c.b_ident = p.buf("ident")
    c.b_ones = p.buf("ones")

    def mk_ident(e):
        e.memset(c.ident_f[:], 0.0)
        return e.affine_select(out=c.ident_f[:], in_=c.ident_f[:], pattern=[[-1, 128]],
                               compare_op=ALU.not_equal, fill=1.0, base=0, channel_multiplier=1)
    p.op("gpsimd", mk_ident, writes=[c.b_ident])
    p.op("vector", lambda e: e.tensor_copy(out=c.ident[:], in_=c.ident_f[:]), reads=[c.b_ident], writes=[c.b_ident])
    p.op("vector", lambda e: e.memset(c.ones[:], 1.0), writes=[c.b_ones])


def prod(sh):
    n = 1
    for s in sh:
        n *= s
    return n


def reshape_free(ap, shape):
    if len(shape) == 1:
        return ap
    names = " ".join(f"a{i}" for i in range(len(shape)))
    kw = {f"a{i}": shape[i] for i in range(1, len(shape))}
    return ap.rearrange(f"p ({names}) -> p {names}", **kw)


class Arena:
    def __init__(self, p, name, nbytes):
        self.n = nbytes // 4
        self.t = p.sb(name, [128, self.n], F32)
        self.off = 0

    def reset(self):
        self.off = 0

    def f32(self, *shape, parts=128):
        n = prod(shape)
        assert self.off + n <= self.n, ("arena overflow", self.off, n, self.n)
        ap = self.t[0:parts, self.off:self.off + n]
        self.off += n
        return reshape_free(ap, shape)

    def i32(self, *shape, parts=128):
        n = prod(shape)
        assert self.off + n <= self.n
        ap = self.t[0:parts, self.off:self.off + n].bitcast(I32)
        self.off += n
        return reshape_free(ap, shape)

    def bf16(self, *shape, parts=128):
        n = prod(shape)
        n32 = (n + 1) // 2
        assert self.off + n32 <= self.n, ("arena overflow", self.off, n32, self.n)
        ap = self.t[0:parts, self.off:self.off + n32].bitcast(BF16)
        self.off += n32
        if n != 2 * n32:
            ap = ap[:, 0:n]
        return reshape_free(ap, shape)


def dram_bcast(ap1d, parts=128):
    n = ap1d.shape[0]
    st = ap1d.ap[-1][0]
    return bass.AP(ap1d.tensor, ap1d.offset, [[0, parts], [st, n]])


def setup_common(p, c):
    c.pb = [p.ps(f"pb{i}", [128, 512], F32) for i in range(8)]
    c.b_pb = p.bufs(8, "pb")
    for b in c.b_pb:
        b.excl = True
    c.ident_f = p.sb("ident_f", [128, 128], F32)
    c.ident = p.sb("ident", [128, 128], BF16)
    c.ones = p.sb("ones", [128, 128], BF16)
    c.b_const = p.buf("const")

    def mk_ident(e):
        e.memset(c.ident_f[:], 0.0)
        return e.affine_select(out=c.ident_f[:], in_=c.ident_f[:], pattern=[[-1, 128]],
                               compare_op=ALU.not_equal, fill=1.0, base=0, channel_multiplier=1)
    p.op("gpsimd", mk_ident, writes=[c.b_const])
    p.op("vector", lambda e: e.tensor_copy(out=c.ident[:], in_=c.ident_f[:]), reads=[c.b_const], writes=[c.b_const])
    p.op("vector", lambda e: e.memset(c.ones[:], 1.0), reads=[c.b_const], writes=[c.b_const])
    c.arena = Arena(p, "arena", 204 * 1024)


def emit_rmsnorm_T(p, c, x_sb, b_x, npart, gain_bc, b_gain, hT, b_hT, col0, scr, hb, b_tmp, ss, rstd, tps_i):
    P_ = npart
    p.op("scalar", lambda e: e.activation(out=scr[0:P_, :], in_=x_sb, func=ACT.Square, accum_out=ss[0:P_, :]),
         reads=[b_x], writes=[b_tmp])
    p.op("vector", lambda e: e.tensor_scalar(out=rstd[0:P_, :], in0=ss[0:P_, :], scalar1=1.0 / D, scalar2=EPS,
                                             op0=ALU.mult, op1=ALU.add), reads=[b_tmp], writes=[b_tmp])
    p.op("scalar", lambda e: e.activation(out=rstd[0:P_, :], in_=rstd[0:P_, :], func=ACT.Sqrt), reads=[b_tmp], writes=[b_tmp])
    p.op("vector", lambda e: e.reciprocal(out=rstd[0:P_, :], in_=rstd[0:P_, :]), reads=[b_tmp], writes=[b_tmp])
    p.op("vector", lambda e: e.scalar_tensor_tensor(out=hb[0:P_, :], in0=x_sb, scalar=rstd[0:P_, :], in1=gain_bc[0:P_, :],
                                                    op0=ALU.mult, op1=ALU.mult), reads=[b_x, b_gain, b_tmp], writes=[b_tmp])
    tp = c.pb[tps_i][:].bitcast(BF16).rearrange("p (k n) -> p k n", n=128)
    for half in range(2):
        for k in range(8):
            kk = half * 8 + k
            p.op("tensor", lambda e, kk=kk, k=k: e.transpose(out=tp[:, k, 0:P_], in_=hb[0:P_, kk * 128:(kk + 1) * 128],
                                                             identity=c.ident[0:P_, 0:P_]),
                 reads=[b_tmp, c.b_const], writes=[c.b_pb[tps_i]])
        p.op("scalar", lambda e, half=half: e.copy(out=hT[:, half * 8:(half + 1) * 8, col0:col0 + P_], in_=tp[:, :, 0:P_]),
             reads=[c.b_pb[tps_i]], writes=[b_hT])


def emit_ffn(p, c, x_in, halo_in, x_out, w_in, conv_w, conv_b, w_out, gain, tag, dbg=None):
    A = c.arena
    A.reset()
    acc = A.f32(4, D)
    gain_bc = A.f32(D)
    hT = A.bf16(DC, TG + 2)
    scr = A.bf16(D)
    hb = A.bf16(D)
    xh = A.f32(D)
    wg = [A.bf16(DC, 512) for _ in range(2)]
    wv = [A.bf16(DC, 512) for _ in range(2)]
    wo = [A.bf16(4, D) for _ in range(2)]
    G = [A.f32(TG + 2) for _ in range(2)]
    cb = [A.f32(TG) for _ in range(2)]
    sl = [A.f32(TG) for _ in range(2)]
    aT = [A.bf16(4, TG) for _ in range(2)]
    cwr = A.f32(4, 128)
    cw = A.f32(4, FC)
    ss = A.f32(8)
    rstd = A.f32(8)

    b_acc = p.bufs(4, "acc"); b_gain = p.buf(); b_hT = p.buf(); b_tmp = p.buf(); b_xh = p.buf()
    b_wg = p.bufs(2); b_wv = p.bufs(2); b_wo = p.bufs(2); b_G = p.bufs(2); b_cb = p.bufs(2); b_sl = p.bufs(2)
    b_aT = p.bufs(2); b_cw = p.buf(); b_xout = p.buf()
    gps = [0, 1]; vps = [2, 3]; hps = 4; ops = [5, 6]; tps = 7

    p.dma("sync", gain_bc, dram_bcast(gain), writes=[b_gain], key="c0")
    for i in range(3):
        p.dma("sync", cwr[0:FC, i, :], conv_w[i, :].rearrange("(j q) -> j q", q=128), writes=[b_cw], key="c1")
    p.dma("sync", cwr[0:FC, 3, :], conv_b.rearrange("(j q) -> j q", q=128), writes=[b_cw], key="c1")
    cwp = c.pb[ops[0]]
    for i in range(4):
        p.op("tensor", lambda e, i=i: e.transpose(out=cwp[:, i * FC:(i + 1) * FC], in_=cwr[0:FC, i, :], identity=c.ident_f[0:FC, 0:FC]),
             reads=[b_cw, c.b_const], writes=[c.b_pb[ops[0]]])
    p.op("vector", lambda e: e.tensor_copy(out=cw.rearrange("p a b -> p (a b)"), in_=cwp[:, 0:4 * FC]),
         reads=[c.b_pb[ops[0]]], writes=[b_cw])

    w_in_v = w_in.rearrange("(k q) n -> q k n", q=128)
    w_out_v = w_out.rearrange("(j q) n -> q j n", q=128)
    NBLK = FC // 4

    def load_in(gi, blk):
        s = (gi * NBLK + blk) % 2
        p.dma("gpsimd", wg[s], w_in_v[:, :, blk * 512:(blk + 1) * 512], writes=[b_wg[s]], key=f"wg{s}")
        p.dma("gpsimd", wv[s], w_in_v[:, :, DFF + blk * 512:DFF + (blk + 1) * 512], writes=[b_wv[s]], key=f"wv{s}")

    def load_out(gi, blk):
        s = (gi * NBLK + blk) % 2
        p.dma("gpsimd", wo[s], w_out_v[:, blk * 4:(blk + 1) * 4, :], writes=[b_wo[s]], key=f"wo{s}")

    for gi in range(NG):
        t0 = gi * TG
        for t in range(4):
            p.dma("sync", acc[:, t, :], x_in[t0 + t * 128:t0 + (t + 1) * 128, :], writes=[b_acc[t]], key=f"x{t}")
        if gi == 0:
            p.dma("sync", xh[0:1, :], halo_in[0:1, :], writes=[b_xh], key="xh")
        else:
            p.dma("sync", xh[0:1, :], x_in[t0 - 1:t0, :], writes=[b_xh], key="xh")
        if gi == NG - 1:
            p.dma("sync", xh[1:2, :], halo_in[1:2, :], writes=[b_xh], key="xh")
        else:
            p.dma("sync", xh[1:2, :], x_in[t0 + TG:t0 + TG + 1, :], writes=[b_xh], key="xh")
        load_in(gi, 0)
        load_out(gi, 0)
        for t in range(4):
            emit_rmsnorm_T(p, c, acc[:, t, :], b_acc[t], 128, gain_bc, b_gain, hT, b_hT, 1 + t * 128, scr, hb, b_tmp,
                           ss[:, t:t + 1], rstd[:, t:t + 1], tps)
        emit_rmsnorm_T_halo(p, c, xh, b_xh, gain_bc, b_gain, hT, b_hT, scr, hb, b_tmp, ss[:, 4:5], rstd[:, 4:5], tps)

        if dbg is not None and gi == 0:
            p.dma("sync", dbg["hT"].rearrange("q (k n) -> q k n", k=DC), hT, reads=[b_hT], writes=[p.buf()], key="dbg")
            p.dma("sync", dbg["cw"], cw.rearrange("p a b -> p (a b)"), reads=[b_cw], writes=[p.buf()], key="dbg")
        pending = None
        for blk in range(NBLK):
            s = (gi * NBLK + blk) % 2
            if blk + 1 < NBLK:
                load_in(gi, blk + 1)
            for i in range(4):
                j = blk * 4 + i
                gs = gps[j % 2]; vs = vps[j % 2]; bs = j % 2
                for k in range(DC):
                    p.op("tensor", lambda e, k=k, i=i, s=s, gs=gs: e.matmul(c.pb[gs][:], lhsT=wg[s][:, k, i * 128:(i + 1) * 128],
                                                                          rhs=hT[:, k, 1:TG + 1], start=(k == 0), stop=(k == DC - 1)),
                         reads=[b_wg[s], b_hT], writes=[c.b_pb[gs]])
                for k in range(DC):
                    p.op("tensor", lambda e, k=k, i=i, s=s: e.matmul(c.pb[hps][:, 0:2], lhsT=wg[s][:, k, i * 128:(i + 1) * 128],
                                                                    rhs=hT[:, k, 0:TG + 2:TG + 1], start=(k == 0), stop=(k == DC - 1)),
                         reads=[b_wg[s], b_hT], writes=[c.b_pb[hps]])
                for k in range(DC):
                    p.op("tensor", lambda e, k=k, i=i, s=s, vs=vs: e.matmul(c.pb[vs][:], lhsT=wv[s][:, k, i * 128:(i + 1) * 128],
                                                                          rhs=hT[:, k, 1:TG + 1], start=(k == 0), stop=(k == DC - 1)),
                         reads=[b_wv[s], b_hT], writes=[c.b_pb[vs]])
                Gb = G[bs]
                p.op("scalar", lambda e, Gb=Gb, gs=gs: e.copy(out=Gb[:, 1:TG + 1], in_=c.pb[gs][:]), reads=[c.b_pb[gs]], writes=[b_G[bs]])
                p.op("scalar", lambda e, Gb=Gb: e.copy(out=Gb[:, 0:TG + 2:TG + 1], in_=c.pb[hps][:, 0:2]), reads=[c.b_pb[hps]], writes=[b_G[bs]])
                p.op("scalar", lambda e, gs=gs, bs=bs, j=j: e.activation(out=cb[bs], in_=c.pb[gs][:], func=ACT.Identity,
                                                                        bias=cw[:, 3, j:j + 1], scale=cw[:, 1, j:j + 1]),
                     reads=[c.b_pb[gs], b_cw], writes=[b_cb[bs]])
                p.op("vector", lambda e, Gb=Gb, bs=bs, j=j: e.scalar_tensor_tensor(out=cb[bs], in0=Gb[:, 0:TG], scalar=cw[:, 0, j:j + 1], in1=cb[bs],
                                                                                  op0=ALU.mult, op1=ALU.add),
                     reads=[b_G[bs], b_cw, b_cb[bs]], writes=[b_cb[bs]])
                p.op("vector", lambda e, Gb=Gb, bs=bs, j=j: e.scalar_tensor_tensor(out=cb[bs], in0=Gb[:, 2:TG + 2], scalar=cw[:, 2, j:j + 1], in1=cb[bs],
                                                                                  op0=ALU.mult, op1=ALU.add),
                     reads=[b_G[bs], b_cw, b_cb[bs]], writes=[b_cb[bs]])
                p.op("scalar", lambda e, bs=bs: e.activation(out=sl[bs], in_=cb[bs], func=ACT.Silu), reads=[b_cb[bs]], writes=[b_sl[bs]])
                p.op("vector", lambda e, bs=bs, s=s, i=i, vs=vs: e.tensor_tensor(out=aT[s][:, i, :], in0=sl[bs], in1=c.pb[vs][:], op=ALU.mult),
                     reads=[b_sl[bs], c.b_pb[vs]], writes=[b_aT[s]])
            if dbg is not None and gi == 0 and blk == 0:
                p.dma("sync", dbg["aT"].rearrange("q (k n) -> q k n", k=4), aT[s], reads=[b_aT[s]], writes=[p.buf()], key="dbg")
                p.dma("sync", dbg["G"], G[1], reads=[b_G[1]], writes=[p.buf()], key="dbg")
            if pending is not None:
                emit_ffn_p2(p, c, pending, aT, b_aT, wo, b_wo, acc, b_acc, ops)
            if blk + 1 < NBLK:
                load_out(gi, blk + 1)
            pending = s
        emit_ffn_p2(p, c, pending, aT, b_aT, wo, b_wo, acc, b_acc, ops)
        for t in range(4):
            p.dma("sync", x_out[t0 + t * 128:t0 + (t + 1) * 128, :], acc[:, t, :], reads=[b_acc[t]], writes=[b_xout], key=f"xo{t}")
    return [f"xo{t}" for t in range(4)]


def emit_ffn_p2(p, c, s, aT, b_aT, wo, b_wo, acc, b_acc, ops):
    n = 0
    for t in range(4):
        for dg in range(4):
            o = ops[n % 2]
            n += 1
            for i in range(4):
                p.op("tensor", lambda e, i=i, t=t, dg=dg, o=o: e.matmul(c.pb[o][:], lhsT=aT[s][:, i, t * 128:(t + 1) * 128],
                                                                        rhs=wo[s][:, i, dg * 512:(dg + 1) * 512], start=(i == 0), stop=(i == 3)),
                     reads=[b_aT[s], b_wo[s]], writes=[c.b_pb[o]])
            p.op("vector", lambda e, t=t, dg=dg, o=o: e.tensor_tensor(out=acc[:, t, dg * 512:(dg + 1) * 512], in0=acc[:, t, dg * 512:(dg + 1) * 512],
                                                                      in1=c.pb[o][:], op=ALU.add),
                 reads=[c.b_pb[o], b_acc[t]], writes=[b_acc[t]])


def emit_rmsnorm_T_halo(p, c, xh, b_xh, gain_bc, b_gain, hT, b_hT, scr, hb, b_tmp, ss, rstd, tps_i, P_=2, places=None):
    if places is None:
        places = [(0, 1, 0), (1, 1, TG + 1)]
    p.op("scalar", lambda e: e.activation(out=scr[0:P_, :], in_=xh[0:P_, :], func=ACT.Square, accum_out=ss[0:P_, :]),
         reads=[b_xh], writes=[b_tmp])
    p.op("vector", lambda e: e.tensor_scalar(out=rstd[0:P_, :], in0=ss[0:P_, :], scalar1=1.0 / D, scalar2=EPS,
                                             op0=ALU.mult, op1=ALU.add), reads=[b_tmp], writes=[b_tmp])
    p.op("scalar", lambda e: e.activation(out=rstd[0:P_, :], in_=rstd[0:P_, :], func=ACT.Sqrt), reads=[b_tmp], writes=[b_tmp])
    p.op("vector", lambda e: e.reciprocal(out=rstd[0:P_, :], in_=rstd[0:P_, :]), reads=[b_tmp], writes=[b_tmp])
    p.op("vector", lambda e: e.scalar_tensor_tensor(out=hb[0:P_, :], in0=xh[0:P_, :], scalar=rstd[0:P_, :], in1=gain_bc[0:P_, :],
                                                    op0=ALU.mult, op1=ALU.mult), reads=[b_xh, b_gain, b_tmp], writes=[b_tmp])
    tp = c.pb[tps_i][:].bitcast(BF16).rearrange("p (k n) -> p k n", n=128)
    for half in range(2):
        for k in range(8):
            kk = half * 8 + k
            p.op("tensor", lambda e, kk=kk, k=k: e.transpose(out=tp[:, k, 0:P_], in_=hb[0:P_, kk * 128:(kk + 1) * 128],
                                                             identity=c.ident[0:P_, 0:P_]),
                 reads=[b_tmp, c.b_const], writes=[c.b_pb[tps_i]])
        for (r0, nr, c0) in places:
            p.op("scalar", lambda e, half=half, r0=r0, nr=nr, c0=c0: e.copy(out=hT[:, half * 8:(half + 1) * 8, c0:c0 + nr], in_=tp[:, :, r0:r0 + nr]),
                 reads=[c.b_pb[tps_i]], writes=[b_hT])


POOL_W = (2, 4, 8, 16)
PH = 8


def emit_pool(p, c, x_ext, valid_ext, x_out, pool_w, pool_scale, gain, ntok):
    A = c.arena
    A.reset()
    TB = 512
    NC_ = TB + 2 * PH
    acc = A.f32(4, D)
    gain_bc = A.f32(D)
    hT = A.bf16(DC, NC_)
    scr = A.bf16(D)
    hb = A.bf16(D)
    xh = A.f32(D)
    wst = A.f32(DC, 512)
    wp = A.bf16(DC, 512)
    sc_bc = A.f32(D)
    vrow = A.f32(NC_)
    cnt = [A.f32(NC_) for _ in range(2)]
    rc = A.f32(4, TB)
    wk = [[A.f32(NC_) for _ in range(2)] for _ in range(2)]
    tmpw = [A.f32(TB) for _ in range(2)]
    pooled = A.bf16(DC, TB)
    ss = A.f32(8)
    rstd = A.f32(8)
    b_acc = p.bufs(4); b_gain = p.buf(); b_hT = p.buf(); b_tmp = p.buf(); b_xh = p.buf(); b_w = p.buf(); b_sc = p.buf()
    b_v = p.buf(); b_cnt = p.bufs(2); b_rc = p.buf(); b_wk = [p.bufs(2), p.bufs(2)]; b_tw = p.bufs(2); b_pl = p.bufs(DC); b_xout = p.buf()
    tps = 7

    p.dma("sync", gain_bc, dram_bcast(gain), writes=[b_gain], key="c0")
    p.dma("sync", sc_bc, dram_bcast(pool_scale), writes=[b_sc], key="c0")
    p.dma("sync", wst, pool_w.rearrange("g (c q) n -> q (g c) n", q=128), writes=[b_w], key="c1")
    for g in range(4):
        for cc in range(4):
            k = g * 4 + cc
            p.op("gpsimd", lambda e, k=k, g=g: e.tensor_tensor(out=wp[:, k, :], in0=wst[:, k, :], in1=sc_bc[:, g * 512:(g + 1) * 512], op=ALU.mult),
                 reads=[b_w, b_sc], writes=[b_w])

    for bi in range(ntok // TB):
        t0 = bi * TB
        for t in range(4):
            p.dma("sync", acc[:, t, :], x_ext[t0 + PH + t * 128:t0 + PH + (t + 1) * 128, :], writes=[b_acc[t]], key=f"x{t}")
        p.dma("sync", xh[0:PH, :], x_ext[t0:t0 + PH, :], writes=[b_xh], key="xh")
        p.dma("sync", xh[PH:2 * PH, :], x_ext[t0 + PH + TB:t0 + 2 * PH + TB, :], writes=[b_xh], key="xh")
        p.dma("sync", vrow, dram_bcast(valid_ext[t0:t0 + NC_]), writes=[b_v], key="vr")
        for t in range(4):
            emit_rmsnorm_T(p, c, acc[:, t, :], b_acc[t], 128, gain_bc, b_gain, hT, b_hT, PH + t * 128, scr, hb, b_tmp,
                           ss[:, t:t + 1], rstd[:, t:t + 1], tps)
        emit_rmsnorm_T_halo(p, c, xh, b_xh, gain_bc, b_gain, hT, b_hT, scr, hb, b_tmp, ss[:, 4:5], rstd[:, 4:5], tps,
                            P_=2 * PH, places=[(0, PH, 0), (PH, PH, PH + TB)])
        prev = vrow; bprev = b_v; n = NC_
        for gi_, w in enumerate(POOL_W):
            hw = w // 2
            dst = cnt[gi_ % 2]; bd = b_cnt[gi_ % 2]
            n2 = n - hw
            p.op("vector", lambda e, dst=dst, prev=prev, n2=n2, hw=hw: e.tensor_tensor(out=dst[:, 0:n2], in0=prev[:, 0:n2], in1=prev[:, hw:hw + n2], op=ALU.add),
                 reads=[bprev], writes=[bd])
            o = PH - w // 2
            p.op("vector", lambda e, dst=dst, o=o, gi_=gi_: e.reciprocal(out=rc[:, gi_, :], in_=dst[:, o:o + TB]), reads=[bd], writes=[b_rc])
            prev = dst; bprev = bd; n = n2
        for k in range(DC):
            g = k // 4
            w = POOL_W[g]
            eng = "vector"
            src_ap = hT[:, k, :]; bsrc = b_hT; n = NC_
            hw = 1; step = 0
            wkk = wk[k % 2]; bwk = b_wk[k % 2]
            while hw < w:
                dst = wkk[step % 2]; bd = bwk[step % 2]
                n2 = n - hw
                p.op(eng, lambda e, dst=dst, src_ap=src_ap, n2=n2, hw=hw: e.tensor_tensor(out=dst[:, 0:n2], in0=src_ap[:, 0:n2], in1=src_ap[:, hw:hw + n2], op=ALU.add),
                     reads=[bsrc], writes=[bd])
                src_ap = dst; bsrc = bd; n = n2; hw *= 2; step += 1
            o = PH - w // 2
            tw = tmpw[k % 2]; btw = b_tw[k % 2]
            p.op(eng, lambda e, tw=tw, src_ap=src_ap, o=o, g=g: e.tensor_tensor(out=tw, in0=src_ap[:, o:o + TB], in1=rc[:, g, :], op=ALU.mult),
                 reads=[bsrc, b_rc], writes=[btw])
            p.op(eng, lambda e, tw=tw, k=k: e.tensor_tensor(out=pooled[:, k, :], in0=tw, in1=hT[:, k, PH:PH + TB], op=ALU.subtract),
                 reads=[btw, b_hT], writes=[b_pl[k]])
        n = 0
        for t in range(4):
            for g in range(4):
                o = n % 4
                n += 1
                for cc in range(4):
                    k = g * 4 + cc
                    p.op("tensor", lambda e, k=k, t=t, o=o, cc=cc: e.matmul(c.pb[o][:], lhsT=pooled[:, k, t * 128:(t + 1) * 128], rhs=wp[:, k, :],
                                                                            start=(cc == 0), stop=(cc == 3)),
                         reads=[b_pl[k], b_w], writes=[c.b_pb[o]])
                p.op("vector", lambda e, t=t, g=g, o=o: e.tensor_tensor(out=acc[:, t, g * 512:(g + 1) * 512], in0=acc[:, t, g * 512:(g + 1) * 512],
                                                                        in1=c.pb[o][:], op=ALU.add),
                     reads=[c.b_pb[o], b_acc[t]], writes=[b_acc[t]])
        for t in range(4):
            p.dma("sync", x_out[bi * TB + t * 128:bi * TB + (t + 1) * 128, :], acc[:, t, :], reads=[b_acc[t]], writes=[b_xout], key=f"xo{t}")
    return [f"xo{t}" for t in range(4)]


MH = 16
MQ = 192
MKV = 256
MSCALE = 192.0 ** -0.5
TWO_PI = 6.283185307179586


def emit_head_rstd(p, ss, rstd, n, dim, b):
    p.op("vector", lambda e: e.tensor_scalar(out=rstd[:, 0:n], in0=ss[:, 0:n], scalar1=1.0 / dim, scalar2=EPS, op0=ALU.mult, op1=ALU.add),
         reads=[b], writes=[b])
    p.op("scalar", lambda e: e.activation(out=rstd[:, 0:n], in_=rstd[:, 0:n], func=ACT.Sqrt), reads=[b], writes=[b])
    p.op("vector", lambda e: e.reciprocal(out=rstd[:, 0:n], in_=rstd[:, 0:n]), reads=[b], writes=[b])


def emit_rope_tables(p, c, A, pos_dram, ntile, inv_bc, b_inv, name, tabs):
    cn, sn = tabs
    pi_ = A.f32(128)
    posf = A.f32(ntile)
    ang = A.f32(ntile, 32)
    nf = A.f32(ntile, 32)
    ni = A.i32(ntile, 32)
    msk = A.f32(ntile, 32)
    b = p.buf(name)
    p.dma("gpsimd", pi_[0:ntile, :], pos_dram.rearrange("(t q) -> t q", q=128), writes=[b], key="c2")
    pp = c.pb[6]
    p.op("tensor", lambda e: e.transpose(out=pp[:, 0:ntile], in_=pi_[0:ntile, :], identity=c.ident_f[0:ntile, 0:ntile]),
         reads=[b, c.b_const], writes=[c.b_pb[6]])
    p.op("vector", lambda e: e.tensor_copy(out=posf, in_=pp[:, 0:ntile]), reads=[c.b_pb[6]], writes=[b])
    for t in range(ntile):
        p.op("vector", lambda e, t=t: e.tensor_scalar(out=ang[:, t, :], in0=inv_bc, scalar1=posf[:, t:t + 1], scalar2=None, op0=ALU.mult),
             reads=[b, b_inv], writes=[b])
    fl = lambda a: a.rearrange("p a b -> p (a b)")
    for which, out_t in ((0, sn), (1, cn)):
        if which == 1:
            p.op("vector", lambda e: e.tensor_scalar(out=fl(ang), in0=fl(ang), scalar1=TWO_PI / 4, scalar2=None, op0=ALU.add), reads=[b], writes=[b])
        p.op("vector", lambda e: e.tensor_scalar(out=fl(nf), in0=fl(ang), scalar1=1.0 / TWO_PI, scalar2=None, op0=ALU.mult), reads=[b], writes=[b])
        p.op("vector", lambda e: e.tensor_copy(out=fl(ni), in_=fl(nf)), reads=[b], writes=[b])
        p.op("vector", lambda e: e.tensor_copy(out=fl(nf), in_=fl(ni)), reads=[b], writes=[b])
        p.op("vector", lambda e: e.scalar_tensor_tensor(out=fl(nf), in0=fl(nf), scalar=-TWO_PI, in1=fl(ang), op0=ALU.mult, op1=ALU.add), reads=[b], writes=[b])
        p.op("vector", lambda e: e.tensor_scalar(out=fl(msk), in0=fl(nf), scalar1=TWO_PI / 2, scalar2=-TWO_PI, op0=ALU.is_gt, op1=ALU.mult), reads=[b], writes=[b])
        p.op("vector", lambda e: e.tensor_tensor(out=fl(nf), in0=fl(nf), in1=fl(msk), op=ALU.add), reads=[b], writes=[b])
        p.op("vector", lambda e: e.tensor_scalar(out=fl(msk), in0=fl(nf), scalar1=-TWO_PI / 2, scalar2=TWO_PI, op0=ALU.is_lt, op1=ALU.mult), reads=[b], writes=[b])
        p.op("vector", lambda e: e.tensor_tensor(out=fl(nf), in0=fl(nf), in1=fl(msk), op=ALU.add), reads=[b], writes=[b])
        p.op("scalar", lambda e, out_t=out_t: e.activation(out=fl(out_t), in_=fl(nf), func=ACT.Sin), reads=[b], writes=[b])
    return cn, sn, b


def emit_rope(p, x, cn_t, sn_t, out, tmp, reads, writes, b_tab):
    x1 = x[:, 0:32]; x2 = x[:, 32:64]
    r = list(reads) + [b_tab]
    p.op("vector", lambda e: e.tensor_tensor(out=tmp[:, 0:32], in0=x2, in1=sn_t, op=ALU.mult), reads=r, writes=writes)
    p.op("vector", lambda e: e.tensor_tensor(out=tmp[:, 32:64], in0=x1, in1=sn_t, op=ALU.mult), reads=r, writes=writes)
    p.op("vector", lambda e: e.tensor_tensor(out=x1, in0=x1, in1=cn_t, op=ALU.mult), reads=r, writes=writes)
    p.op("vector", lambda e: e.tensor_tensor(out=x2, in0=x2, in1=cn_t, op=ALU.mult), reads=r, writes=writes)
    p.op("vector", lambda e: e.tensor_tensor(out=out[:, 0:32], in0=x1, in1=tmp[:, 0:32], op=ALU.subtract), reads=r, writes=writes)
    p.op("vector", lambda e: e.tensor_tensor(out=out[:, 32:64], in0=x2, in1=tmp[:, 32:64], op=ALU.add), reads=r, writes=writes)


def emit_mla(p, c, x_seq, x_own, pos_seq, pos_own, rope_inv, x_out, w_down, q_a_gain, kv_a_gain, w_uq, w_ukv,
             qn_gain, qr_gain, kn_gain, kr_gain, w_o, gain, nseq, nown, stop=None):
    A = c.arena
    A.reset()
    NKT = nseq // 128
    NQT = nown // 128
    NQG = nown // 512
    ckvT = A.bf16(4, nseq)
    kpeT = A.bf16(nseq)
    cqT = A.bf16(4, nown)
    OT = A.bf16(MH, nown)
    inv_bc = A.f32(32)
    hg = A.f32(384)
    b_inv = p.buf(); b_hg = p.buf(); b_ckvT = p.buf(); b_kpeT = p.buf(); b_cqT = p.buf(); b_OT = p.bufs(NQG)
    p.dma("sync", inv_bc, dram_bcast(rope_inv), writes=[b_inv], key="c0")
    p.dma("sync", hg[:, 0:128], dram_bcast(qn_gain), writes=[b_hg], key="c0")
    p.dma("sync", hg[:, 128:192], dram_bcast(qr_gain), writes=[b_hg], key="c0")
    p.dma("sync", hg[:, 192:320], dram_bcast(kn_gain), writes=[b_hg], key="c0")
    p.dma("sync", hg[:, 320:384], dram_bcast(kr_gain), writes=[b_hg], key="c0")
    p.op("vector", lambda e: e.tensor_scalar(out=hg[:, 0:192], in0=hg[:, 0:192], scalar1=MSCALE, scalar2=None, op0=ALU.mult), reads=[b_hg], writes=[b_hg])
    tab_s = (A.f32(NKT, 32), A.f32(NKT, 32))
    tab_o = (A.f32(NQT, 32), A.f32(NQT, 32))
    mark = A.off
    cn_s, sn_s, b_ts = emit_rope_tables(p, c, A, pos_seq, NKT, inv_bc, b_inv, "ts", tab_s)
    cn_o, sn_o, b_to = emit_rope_tables(p, c, A, pos_own, NQT, inv_bc, b_inv, "to", tab_o)
    if stop == "T":
        return []
    p.barrier()
    A.off = mark

    xt = [A.f32(D)] * 2
    gain_bc = A.f32(D)
    ga_bc = A.f32(1024)
    scr = A.bf16(D); hb = A.bf16(D)
    hTt = [A.bf16(DC, 128) for _ in range(2)]
    wd = A.bf16(DC, 576)
    cbn = [A.bf16(512) for _ in range(2)]
    krf = A.f32(64); krt = A.f32(64); krb = A.bf16(64)
    ss = A.f32(8); rstd = A.f32(8)
    b_xt = [p.buf()] * 2; b_gain = p.buf(); b_ga = p.buf(); b_tmp = p.buf(); b_hTt = p.bufs(2); b_wd = p.buf(); b_cbn = p.bufs(2); b_kr = p.buf(); b_st = p.buf()
    p.dma("sync", gain_bc, dram_bcast(gain), writes=[b_gain], key="c0")
    p.dma("sync", ga_bc[:, 0:512], dram_bcast(q_a_gain), writes=[b_ga], key="c0")
    p.dma("sync", ga_bc[:, 512:1024], dram_bcast(kv_a_gain), writes=[b_ga], key="c0")
    p.dma("gpsimd", wd, w_down.rearrange("(k q) n -> q k n", q=128)[:, :, 512:1088], writes=[b_wd], key="wd")
    tps = 7
    tp = c.pb[tps][:].bitcast(BF16).rearrange("p (k n) -> p k n", n=128)

    def down(tile_i, src, is_kv):
        s = tile_i % 2
        p.dma("sync", xt[s], src[tile_i * 128:(tile_i + 1) * 128, :], writes=[b_xt[s]], key=f"x{s}")
        emit_rmsnorm_T(p, c, xt[s], b_xt[s], 128, gain_bc, b_gain, hTt[s], b_hTt[s], 0, scr, hb, b_tmp, ss[:, 0:1], rstd[:, 0:1], tps)
        for k in range(DC):
            p.op("tensor", lambda e, k=k: e.matmul(c.pb[4][:], lhsT=hTt[s][:, k, :], rhs=wd[:, k, 0:512], start=(k == 0), stop=(k == DC - 1)),
                 reads=[b_hTt[s], b_wd], writes=[c.b_pb[4]])
        if is_kv:
            for k in range(DC):
                p.op("tensor", lambda e, k=k: e.matmul(c.pb[5][:, 0:64], lhsT=hTt[s][:, k, :], rhs=wd[:, k, 512:576], start=(k == 0), stop=(k == DC - 1)),
                     reads=[b_hTt[s], b_wd], writes=[c.b_pb[5]])
        p.op("scalar", lambda e: e.activation(out=scr[:, 0:512], in_=c.pb[4][:], func=ACT.Square, accum_out=ss[:, 1:2]), reads=[c.b_pb[4]], writes=[b_st])
        if is_kv:
            p.op("scalar", lambda e: e.activation(out=scr[:, 512:576], in_=c.pb[5][:, 0:64], func=ACT.Square, accum_out=ss[:, 2:3]), reads=[c.b_pb[5]], writes=[b_st])
        p.op("vector", lambda e: e.tensor_scalar(out=rstd[:, 1:2], in0=ss[:, 1:2], scalar1=1.0 / 512, scalar2=EPS, op0=ALU.mult, op1=ALU.add), reads=[b_st], writes=[b_st])
        p.op("vector", lambda e: e.tensor_scalar(out=rstd[:, 2:3], in0=ss[:, 2:3], scalar1=1.0 / 64, scalar2=EPS, op0=ALU.mult, op1=ALU.add), reads=[b_st], writes=[b_st])
        p.op("scalar", lambda e: e.activation(out=rstd[:, 1:3], in_=rstd[:, 1:3], func=ACT.Sqrt), reads=[b_st], writes=[b_st])
        p.op("vector", lambda e: e.reciprocal(out=rstd[:, 1:3], in_=rstd[:, 1:3]), reads=[b_st], writes=[b_st])
        g0 = 512 if is_kv else 0
        p.op("vector", lambda e: e.scalar_tensor_tensor(out=cbn[s], in0=c.pb[4][:], scalar=rstd[:, 1:2], in1=ga_bc[:, g0:g0 + 512], op0=ALU.mult, op1=ALU.mult),
             reads=[c.b_pb[4], b_st, b_ga], writes=[b_cbn[s]])
        for k in range(4):
            p.op("tensor", lambda e, k=k: e.transpose(out=tp[:, k, :], in_=cbn[s][:, k * 128:(k + 1) * 128], identity=c.ident[:]),
                 reads=[b_cbn[s], c.b_const], writes=[c.b_pb[tps]])
        dstT = ckvT if is_kv else cqT
        bdst = b_ckvT if is_kv else b_cqT
        p.op("scalar", lambda e: e.copy(out=dstT[:, :, tile_i * 128:(tile_i + 1) * 128], in_=tp[:, 0:4, :]), reads=[c.b_pb[tps]], writes=[bdst])
        if is_kv:
            p.op("vector", lambda e: e.scalar_tensor_tensor(out=krf, in0=c.pb[5][:, 0:64], scalar=rstd[:, 2:3], in1=hg[:, 320:384], op0=ALU.mult, op1=ALU.mult),
                 reads=[c.b_pb[5], b_st, b_hg], writes=[b_kr])
            emit_rope(p, krf, cn_s[:, tile_i, :], sn_s[:, tile_i, :], krb, krt, [b_kr], [b_kr], b_ts)
            p.op("tensor", lambda e: e.transpose(out=tp[0:64, 4, :], in_=krb, identity=c.ident[:]), reads=[b_kr, c.b_const], writes=[c.b_pb[tps]])
            p.op("scalar", lambda e: e.copy(out=kpeT[0:64, tile_i * 128:(tile_i + 1) * 128], in_=tp[0:64, 4, :]), reads=[c.b_pb[tps]], writes=[b_kpeT])

    for ti in range(NKT):
        down(ti, x_seq, True)
    p.dma("gpsimd", wd[:, :, 0:512], w_down.rearrange("(k q) n -> q k n", q=128)[:, :, 0:512], writes=[b_wd], key="wd")
    for ti in range(NQT):
        down(ti, x_own, False)

    if stop == "D":
        return []
    p.barrier()
    A.off = mark
    wuq = [A.bf16(4, MQ) for _ in range(2)]
    wukv = [A.bf16(4, MKV) for _ in range(2)]
    kraw = A.f32(NKT, 128)
    Vh = A.bf16(NKT, 128)
    KTh = A.bf16(nseq)
    qraw = A.f32(NQT, MQ)
    QTn = A.bf16(nown)
    QTr = A.bf16(nown)
    knb = [A.bf16(128) for _ in range(2)]
    qnb = [A.bf16(128) for _ in range(2)]
    qrf = A.f32(64); qrt = A.f32(64); qrb = [A.bf16(64) for _ in range(2)]
    PT = [A.bf16(512) for _ in range(3)]
    rden = A.f32(512)
    scr2 = A.f32(256)
    ssk = A.f32(NKT); rsk = A.f32(NKT); ssn = A.f32(NQT); rsn = A.f32(NQT); ssr = A.f32(NQT); rsr = A.f32(NQT)
    b_wuq = p.bufs(2); b_wukv = p.bufs(2); b_kraw = p.buf(); b_Vh = p.buf(); b_KTh = p.buf(); b_qraw = p.buf(); b_QTn = p.buf(); b_QTr = p.buf()
    b_knb = p.bufs(2); b_qnb = p.bufs(2); b_qr = p.buf(); b_qrb = p.bufs(2); b_PT = p.bufs(3); b_rden = p.buf(); b_scr2 = p.buf()
    b_ssk = p.buf(); b_ssq = p.buf()
    w_uq_v = w_uq.rearrange("(k q) n -> q k n", q=128)
    w_ukv_v = w_ukv.rearrange("(k q) n -> q k n", q=128)

    def load_head(h):
        s = h % 2
        p.dma("gpsimd", wuq[s], w_uq_v[:, :, h * MQ:(h + 1) * MQ], writes=[b_wuq[s]], key=f"wq{s}")
        p.dma("gpsimd", wukv[s], w_ukv_v[:, :, h * MKV:(h + 1) * MKV], writes=[b_wukv[s]], key=f"wk{s}")

    load_head(0)
    if stop == "HL":
        return []
    npt = 0
    for h in range(MH):
        s = h % 2
        if h + 1 < MH:
            load_head(h + 1)
        for kt in range(NKT):
            o = 4 + kt % 2
            for k in range(4):
                p.op("tensor", lambda e, k=k, kt=kt, o=o, s=s: e.matmul(c.pb[o][:, 0:MKV], lhsT=ckvT[:, k, kt * 128:(kt + 1) * 128], rhs=wukv[s][:, k, :],
                                                                   start=(k == 0), stop=(k == 3)),
                     reads=[b_ckvT, b_wukv[s]], writes=[c.b_pb[o]])
            dbgm = "abc"
            if "a" in dbgm:
                p.op("scalar", lambda e, kt=kt, o=o: e.activation(out=scr2[:, 0:128], in_=c.pb[o][:, 0:128], func=ACT.Square, accum_out=ssk[:, kt:kt + 1]),
                     reads=[c.b_pb[o]], writes=[b_scr2, b_ssk])
            if "b" in dbgm:
                p.op("vector", lambda e, kt=kt, o=o: e.tensor_copy(out=kraw[:, kt, :], in_=c.pb[o][:, 0:128]), reads=[c.b_pb[o]], writes=[b_kraw])
            if "c" in dbgm:
                p.op("scalar", lambda e, kt=kt, o=o: e.copy(out=Vh[:, kt, :], in_=c.pb[o][:, 128:256]), reads=[c.b_pb[o]], writes=[b_Vh])
        if stop == "HK1":
            return []
        emit_head_rstd(p, ssk, rsk, NKT, 128, b_ssk)
        if stop == "HK2":
            return []
        for kt in range(NKT):
            kb = knb[kt % 2]; bkb = b_knb[kt % 2]
            p.op("vector", lambda e, kt=kt, kb=kb: e.scalar_tensor_tensor(out=kb, in0=kraw[:, kt, :], scalar=rsk[:, kt:kt + 1], in1=hg[:, 192:320],
                                                                          op0=ALU.mult, op1=ALU.mult),
                 reads=[b_kraw, b_ssk, b_hg], writes=[bkb])
            p.op("tensor", lambda e, kt=kt, kb=kb: e.transpose(out=tp[:, kt % 4, :], in_=kb, identity=c.ident[:]), reads=[bkb, c.b_const], writes=[c.b_pb[tps]])
            if kt % 4 == 3:
                p.op("scalar", lambda e, kt=kt: e.copy(out=KTh[:, (kt - 3) * 128:(kt + 1) * 128], in_=tp[:, 0:4, :].rearrange("p a b -> p (a b)")),
                     reads=[c.b_pb[tps]], writes=[b_KTh])
        if stop == "HK":
            return []
        for qt in range(NQT):
            o = 4 + qt % 2
            for k in range(4):
                p.op("tensor", lambda e, k=k, qt=qt, o=o, s=s: e.matmul(c.pb[o][:, 0:MQ], lhsT=cqT[:, k, qt * 128:(qt + 1) * 128], rhs=wuq[s][:, k, :],
                                                                   start=(k == 0), stop=(k == 3)),
                     reads=[b_cqT, b_wuq[s]], writes=[c.b_pb[o]])
            p.op("scalar", lambda e, qt=qt, o=o: e.activation(out=scr2[:, 0:128], in_=c.pb[o][:, 0:128], func=ACT.Square, accum_out=ssn[:, qt:qt + 1]),
                 reads=[c.b_pb[o]], writes=[b_scr2, b_ssq])
            p.op("scalar", lambda e, qt=qt, o=o: e.activation(out=scr2[:, 128:192], in_=c.pb[o][:, 128:192], func=ACT.Square, accum_out=ssr[:, qt:qt + 1]),
                 reads=[c.b_pb[o]], writes=[b_scr2, b_ssq])
            p.op("vector", lambda e, qt=qt, o=o: e.tensor_copy(out=qraw[:, qt, :], in_=c.pb[o][:, 0:MQ]), reads=[c.b_pb[o]], writes=[b_qraw])
        emit_head_rstd(p, ssn, rsn, NQT, 128, b_ssq)
        emit_head_rstd(p, ssr, rsr, NQT, 64, b_ssq)
        for qt in range(NQT):
            qb = qnb[qt % 2]; bqb = b_qnb[qt % 2]
            p.op("vector", lambda e, qt=qt, qb=qb: e.scalar_tensor_tensor(out=qb, in0=qraw[:, qt, 0:128], scalar=rsn[:, qt:qt + 1], in1=hg[:, 0:128],
                                                                          op0=ALU.mult, op1=ALU.mult),
                 reads=[b_qraw, b_ssq, b_hg], writes=[bqb])
            p.op("tensor", lambda e, qt=qt, qb=qb: e.transpose(out=tp[:, qt % 4, :], in_=qb, identity=c.ident[:]), reads=[bqb, c.b_const], writes=[c.b_pb[tps]])
            p.op("vector", lambda e, qt=qt: e.scalar_tensor_tensor(out=qrf, in0=qraw[:, qt, 128:192], scalar=rsr[:, qt:qt + 1], in1=hg[:, 128:192],
                                                                   op0=ALU.mult, op1=ALU.mult),
                 reads=[b_qraw, b_ssq, b_hg], writes=[b_qr])
            rb = qrb[qt % 2]; brb = b_qrb[qt % 2]
            emit_rope(p, qrf, cn_o[:, qt, :], sn_o[:, qt, :], rb, qrt, [b_qr], [b_qr, brb], b_to)
            p.op("tensor", lambda e, qt=qt, rb=rb: e.transpose(out=tp[0:64, 4 + qt % 4, :], in_=rb, identity=c.ident[:]), reads=[brb, c.b_const], writes=[c.b_pb[tps]])
            if qt % 4 == 3:
                p.op("scalar", lambda e, qt=qt: e.copy(out=QTn[:, (qt - 3) * 128:(qt + 1) * 128], in_=tp[:, 0:4, :].rearrange("p a b -> p (a b)")),
                     reads=[c.b_pb[tps]], writes=[b_QTn])
                p.op("scalar", lambda e, qt=qt: e.copy(out=QTr[0:64, (qt - 3) * 128:(qt + 1) * 128], in_=tp[0:64, 4:8, :].rearrange("p a b -> p (a b)")),
                     reads=[c.b_pb[tps]], writes=[b_QTr])
        if stop == "HQ":
            return []
        for qg in range(NQG):
            qs = slice(qg * 512, (qg + 1) * 512)
            for kt in range(NKT):
                o = kt % 2
                ks = slice(kt * 128, (kt + 1) * 128)
                p.op("tensor", lambda e, o=o, ks=ks, qs=qs: e.matmul(c.pb[o][:], lhsT=KTh[:, ks], rhs=QTn[:, qs], start=True, stop=False),
                     reads=[b_KTh, b_QTn], writes=[c.b_pb[o]])
                p.op("tensor", lambda e, o=o, ks=ks, qs=qs: e.matmul(c.pb[o][:], lhsT=kpeT[0:64, ks], rhs=QTr[0:64, qs], start=False, stop=True),
                     reads=[b_kpeT, b_QTr], writes=[c.b_pb[o]])
                pt = PT[npt % 3]; bpt = b_PT[npt % 3]
                npt += 1
                p.op("scalar", lambda e, o=o, pt=pt: e.activation(out=pt, in_=c.pb[o][:], func=ACT.Exp), reads=[c.b_pb[o]], writes=[bpt])
                p.op("tensor", lambda e, kt=kt, pt=pt: e.matmul(c.pb[2][:], lhsT=Vh[:, kt, :], rhs=pt, start=(kt == 0), stop=(kt == NKT - 1)),
                     reads=[b_Vh, bpt], writes=[c.b_pb[2]])
                p.op("tensor", lambda e, kt=kt, pt=pt: e.matmul(c.pb[3][:], lhsT=c.ones[:], rhs=pt, start=(kt == 0), stop=(kt == NKT - 1)),
                     reads=[c.b_const, bpt], writes=[c.b_pb[3]])
            p.op("vector", lambda e: e.reciprocal(out=rden, in_=c.pb[3][:]), reads=[c.b_pb[3]], writes=[b_rden])
            p.op("vector", lambda e, qs=qs, h=h: e.tensor_tensor(out=OT[:, h, qs], in0=c.pb[2][:], in1=rden, op=ALU.mult),
                 reads=[c.b_pb[2], b_rden], writes=[b_OT[qg]])

    if stop == "H":
        return []
    p.barrier()
    A.off = mark
    wo = [A.bf16(MH, 512) for _ in range(2)]
    acc = [A.f32(512) for _ in range(4)]
    b_wo = p.bufs(2); b_acc = p.bufs(4); b_xout = p.buf()
    w_o_v = w_o.rearrange("(h q) n -> q h n", q=128)
    n = 0
    for dg in range(4):
        ws = dg % 2
        p.dma("gpsimd", wo[ws], w_o_v[:, :, dg * 512:(dg + 1) * 512], writes=[b_wo[ws]], key=f"wq{ws}")
        for t in range(NQT):
            s = n % 4
            o = 4 + n % 2
            n += 1
            p.dma("sync", acc[s], x_own[t * 128:(t + 1) * 128, dg * 512:(dg + 1) * 512], writes=[b_acc[s]], key=f"xa{s}")
            for h in range(MH):
                p.op("tensor", lambda e, h=h, t=t, ws=ws, o=o: e.matmul(c.pb[o][:], lhsT=OT[:, h, t * 128:(t + 1) * 128], rhs=wo[ws][:, h, :],
                                                                        start=(h == 0), stop=(h == MH - 1)),
                     reads=[b_OT[t // 4], b_wo[ws]], writes=[c.b_pb[o]])
            p.op("vector", lambda e, s=s, o=o: e.tensor_tensor(out=acc[s], in0=acc[s], in1=c.pb[o][:], op=ALU.add),
                 reads=[c.b_pb[o], b_acc[s]], writes=[b_acc[s]])
            p.dma("sync", x_out[t * 128:(t + 1) * 128, dg * 512:(dg + 1) * 512], acc[s], reads=[b_acc[s]], writes=[b_xout], key=f"xo{s}")
    return [f"xo{s}" for s in range(4)]


SW = 128
SH = 32
SKV = 4
SD = 64
BIGM = 1.0e6
SLOPES = [2.0 ** (-8.0 * (h + 1) / SH) for h in range(SH)]


def emit_swa(p, c, x_ext, kvalid, pos_ext, x_out, w_qkv, q_gain, k_gain, sinks, w_o, gain, ntok):
    A = c.arena
    A.reset()
    TB = 512
    NE = TB + 2 * SW
    acc = A.f32(4, D)
    hT = A.bf16(DC, NE)
    xt = A.f32(D)
    gain_bc = A.f32(D)
    scr = A.bf16(D); hb = A.bf16(D)
    KT = A.bf16(SKV, NE)
    Vv = A.bf16(6, 256)
    wkv = A.bf16(DC, 512)
    wq = A.bf16(DC, 512)
    wo = A.bf16(8, D)
    QT = A.bf16(8, TB)
    OT = A.bf16(8, TB)
    qnb = A.bf16(512)
    knb = A.bf16(256)
    scrf = A.f32(512)
    Ssb = [A.f32(8, 128) for _ in range(2)]
    PT = [A.bf16(1024) for _ in range(3)]
    distm = A.f32(12, 128)
    posq = A.f32(NE)
    posk = A.f32(8); kinv = A.f32(8); negk = A.f32(8)
    Mlo = A.f32(128); Mhi = A.f32(128)
    rd = A.f32(1024)
    qg_bc = A.f32(64); kg_bc = A.f32(64); esink = A.f32(32)
    ss = A.f32(16); rstd = A.f32(16); ssh = A.f32(8); rsh = A.f32(8)
    b_acc = p.bufs(4); b_hT = p.buf(); b_xt = p.buf(); b_gain = p.buf(); b_tmp = p.buf(); b_KT = p.buf(); b_V = p.buf()
    b_wkv = p.buf(); b_wq = p.buf(); b_wo = p.buf(); b_QT = p.buf(); b_OT = p.buf(); b_qnb = p.buf(); b_knb = p.buf(); b_scrf = p.buf()
    b_S = p.bufs(2); b_PT = p.bufs(3); b_dm = p.buf(); b_pos = p.buf(); b_M = p.buf(); b_rd = p.buf(); b_g = p.buf(); b_sh = p.buf(); b_xout = p.buf()
    tps = 7
    tp = c.pb[tps][:].bitcast(BF16).rearrange("p (k n) -> p k n", n=128)

    p.dma("sync", gain_bc, dram_bcast(gain), writes=[b_gain], key="c0")
    p.dma("sync", qg_bc, dram_bcast(q_gain), writes=[b_g], key="c0")
    p.dma("sync", kg_bc, dram_bcast(k_gain), writes=[b_g], key="c0")
    p.dma("sync", esink, dram_bcast(sinks), writes=[b_g], key="c0")
    p.op("vector", lambda e: e.tensor_scalar(out=qg_bc, in0=qg_bc, scalar1=SD ** -0.5, scalar2=None, op0=ALU.mult), reads=[b_g], writes=[b_g])
    p.op("scalar", lambda e: e.activation(out=esink, in_=esink, func=ACT.Exp), reads=[b_g], writes=[b_g])

    def mk_masks(e):
        e.memset(Mlo, 0.0)
        e.memset(Mhi, 0.0)
        e.affine_select(out=Mlo, in_=Mlo, pattern=[[-1, 128]], compare_op=ALU.is_ge, fill=BIGM, base=0, channel_multiplier=1)
        return e.affine_select(out=Mhi, in_=Mhi, pattern=[[1, 128]], compare_op=ALU.is_ge, fill=BIGM, base=0, channel_multiplier=-1)
    p.op("gpsimd", mk_masks, writes=[b_M])

    w_v = w_qkv.rearrange("(k q) n -> q k n", q=128)
    w_o_v = w_o.rearrange("(h q) n -> q h n", q=SD)

    def head_norm(src_ps, b_src, nh, g_bc, dst, b_dst):
        n = nh * SD
        p.op("scalar", lambda e: e.activation(out=scrf[:, 0:n], in_=src_ps[:, 0:n], func=ACT.Square), reads=[b_src], writes=[b_scrf])
        p.op("vector", lambda e: e.tensor_reduce(out=ssh[:, 0:nh], in_=scrf[:, 0:n].rearrange("p (h d) -> p h d", d=SD), axis=AX.X, op=ALU.add),
             reads=[b_scrf], writes=[b_sh])
        p.op("vector", lambda e: e.tensor_scalar(out=rsh[:, 0:nh], in0=ssh[:, 0:nh], scalar1=1.0 / SD, scalar2=EPS, op0=ALU.mult, op1=ALU.add), reads=[b_sh], writes=[b_sh])
        p.op("scalar", lambda e: e.activation(out=rsh[:, 0:nh], in_=rsh[:, 0:nh], func=ACT.Sqrt), reads=[b_sh], writes=[b_sh])
        p.op("vector", lambda e: e.reciprocal(out=rsh[:, 0:nh], in_=rsh[:, 0:nh]), reads=[b_sh], writes=[b_sh])
        for h in range(nh):
            p.op("vector", lambda e, h=h: e.scalar_tensor_tensor(out=dst[:, h * SD:(h + 1) * SD], in0=src_ps[:, h * SD:(h + 1) * SD], scalar=rsh[:, h:h + 1], in1=g_bc,
                                                                 op0=ALU.mult, op1=ALU.mult), reads=[b_src, b_sh, b_g], writes=[b_dst])

    def block(bi):
        t0 = bi * TB
        for e_ in range(6):
            rows = slice(t0 + e_ * 128, t0 + (e_ + 1) * 128)
            if 1 <= e_ <= 4:
                xs = acc[:, e_ - 1, :]; bx = b_acc[e_ - 1]
                p.dma("sync", xs, x_ext[rows, :], writes=[bx], key=f"x{e_}")
            else:
                xs = xt; bx = b_xt
                p.dma("sync", xs, x_ext[rows, :], writes=[bx], key="xh")
            emit_rmsnorm_T(p, c, xs, bx, 128, gain_bc, b_gain, hT, b_hT, e_ * 128, scr, hb, b_tmp, ss[:, 0:1], rstd[:, 0:1], tps)
        p.dma("gpsimd", posq, dram_bcast(pos_ext[t0:t0 + NE]), writes=[b_pos], key="pq")
        for e_ in range(6):
            p.dma("gpsimd", posk[:, e_:e_ + 1], pos_ext[t0 + e_ * 128:t0 + (e_ + 1) * 128].rearrange("(q o) -> q o", o=1), writes=[b_pos], key="pq")
            p.dma("sync", kinv[:, e_:e_ + 1], kvalid[t0 + e_ * 128:t0 + (e_ + 1) * 128].rearrange("(q o) -> q o", o=1), writes=[b_pos], key="pv")
        p.op("vector", lambda e: e.tensor_scalar(out=kinv[:, 0:6], in0=kinv[:, 0:6], scalar1=-BIGM, scalar2=BIGM, op0=ALU.mult, op1=ALU.add), reads=[b_pos], writes=[b_pos])
        p.op("vector", lambda e: e.tensor_scalar(out=negk[:, 0:6], in0=posk[:, 0:6], scalar1=-1.0, scalar2=None, op0=ALU.mult), reads=[b_pos], writes=[b_pos])
        for t in range(4):
            for r in range(3):
                e_ = t + r
                pi = t * 3 + r
                dm = distm[:, pi, :]
                p.op("scalar", lambda e, dm=dm, t=t, e_=e_: e.activation(out=dm, in_=posq[:, (t + 1) * 128:(t + 2) * 128], func=ACT.Abs,
                                                                        bias=negk[:, e_:e_ + 1], scale=1.0), reads=[b_pos], writes=[b_dm])
                if r != 1:
                    M = Mlo if r == 0 else Mhi
                    p.op("vector", lambda e, dm=dm, M=M: e.tensor_tensor(out=dm, in0=dm, in1=M, op=ALU.add), reads=[b_M, b_dm], writes=[b_dm])
                p.op("vector", lambda e, dm=dm, e_=e_: e.tensor_scalar(out=dm, in0=dm, scalar1=kinv[:, e_:e_ + 1], scalar2=None, op0=ALU.add), reads=[b_pos, b_dm], writes=[b_dm])
        p.dma("gpsimd", wkv, w_v[:, :, 2048:2560], writes=[b_wkv], key="wk")
        for e_ in range(6):
            for k in range(DC):
                p.op("tensor", lambda e, k=k, e_=e_: e.matmul(c.pb[4][:], lhsT=hT[:, k, e_ * 128:(e_ + 1) * 128], rhs=wkv[:, k, :], start=(k == 0), stop=(k == DC - 1)),
                     reads=[b_hT, b_wkv], writes=[c.b_pb[4]])
            head_norm(c.pb[4], c.b_pb[4], SKV, kg_bc, knb, b_knb)
            p.op("vector", lambda e, e_=e_: e.tensor_copy(out=Vv[:, e_, :], in_=c.pb[4][:, 256:512]), reads=[c.b_pb[4]], writes=[b_V])
            for h in range(SKV):
                p.op("tensor", lambda e, h=h: e.transpose(out=tp[0:SD, h, :], in_=knb[:, h * SD:(h + 1) * SD], identity=c.ident[:]),
                     reads=[b_knb, c.b_const], writes=[c.b_pb[tps]])
            p.op("scalar", lambda e, e_=e_: e.copy(out=KT[0:SD, :, e_ * 128:(e_ + 1) * 128], in_=tp[0:SD, 0:SKV, :]), reads=[c.b_pb[tps]], writes=[b_KT])
        npt = [0]
        for g in range(SKV):
            p.dma("gpsimd", wq, w_v[:, :, g * 512:(g + 1) * 512], writes=[b_wq], key="wq")
            p.dma("gpsimd", wo[0:SD, :, :], w_o_v[:, g * 8:(g + 1) * 8, :], writes=[b_wo], key="wo")
            for t in range(4):
                for k in range(DC):
                    p.op("tensor", lambda e, k=k, t=t: e.matmul(c.pb[4][:], lhsT=hT[:, k, (t + 1) * 128:(t + 2) * 128], rhs=wq[:, k, :], start=(k == 0), stop=(k == DC - 1)),
                         reads=[b_hT, b_wq], writes=[c.b_pb[4]])
                head_norm(c.pb[4], c.b_pb[4], 8, qg_bc, qnb, b_qnb)
                for h in range(8):
                    p.op("tensor", lambda e, h=h: e.transpose(out=tp[0:SD, h, :], in_=qnb[:, h * SD:(h + 1) * SD], identity=c.ident[:]),
                         reads=[b_qnb, c.b_const], writes=[c.b_pb[tps]])
                p.op("scalar", lambda e, t=t: e.copy(out=QT[0:SD, :, t * 128:(t + 1) * 128], in_=tp[0:SD, 0:8, :]), reads=[c.b_pb[tps]], writes=[b_QT])
            for t in range(4):
                for r in range(3):
                    e_ = t + r
                    pi = t * 3 + r
                    for half in range(2):
                        p.op("tensor", lambda e, half=half, e_=e_, t=t, g=g: e.matmul(c.pb[half][:], lhsT=KT[0:SD, g, e_ * 128:(e_ + 1) * 128],
                                                                                     rhs=QT[0:SD, half * 4:(half + 1) * 4, t * 128:(t + 1) * 128], start=True, stop=True),
                             reads=[b_KT, b_QT], writes=[c.b_pb[half]])
                    sb_ = Ssb[pi % 2]; bsb = b_S[pi % 2]
                    for hh in range(8):
                        sl = -SLOPES[g * 8 + hh]
                        p.op("vector", lambda e, hh=hh, sl=sl, pi=pi, sb_=sb_: e.scalar_tensor_tensor(out=sb_[:, hh, :], in0=distm[:, pi, :], scalar=sl,
                                                                                                     in1=c.pb[hh // 4][:, (hh % 4) * 128:(hh % 4 + 1) * 128],
                                                                                                     op0=ALU.mult, op1=ALU.add),
                             reads=[b_dm, c.b_pb[hh // 4]], writes=[bsb])
                    pt = PT[npt[0] % 3]; bpt = b_PT[npt[0] % 3]
                    npt[0] += 1
                    p.op("scalar", lambda e, pt=pt, sb_=sb_: e.activation(out=pt, in_=sb_.rearrange("p a b -> p (a b)"), func=ACT.Exp), reads=[bsb], writes=[bpt])
                    for half in range(2):
                        p.op("tensor", lambda e, half=half, e_=e_, g=g, pt=pt, r=r: e.matmul(c.pb[2 + half][0:SD, :], lhsT=Vv[:, e_, g * SD:(g + 1) * SD],
                                                                                           rhs=pt[:, half * 512:(half + 1) * 512], start=(r == 0), stop=(r == 2)),
                             reads=[b_V, bpt], writes=[c.b_pb[2 + half]])
                        p.op("tensor", lambda e, half=half, pt=pt, r=r: e.matmul(c.pb[5 + half][0:SD, :], lhsT=c.ones[:, 0:SD],
                                                                               rhs=pt[:, half * 512:(half + 1) * 512], start=(r == 0), stop=(r == 2)),
                             reads=[c.b_const, bpt], writes=[c.b_pb[5 + half]])
                for hh in range(8):
                    hg_ = g * 8 + hh
                    p.op("vector", lambda e, hh=hh, hg_=hg_: e.tensor_scalar(out=rd[0:SD, hh * 128:(hh + 1) * 128], in0=c.pb[5 + hh // 4][0:SD, (hh % 4) * 128:(hh % 4 + 1) * 128],
                                                                          scalar1=esink[0:SD, hg_:hg_ + 1], scalar2=None, op0=ALU.add),
                         reads=[c.b_pb[5 + hh // 4], b_g], writes=[b_rd])
                p.op("vector", lambda e: e.reciprocal(out=rd[0:SD, :], in_=rd[0:SD, :]), reads=[b_rd], writes=[b_rd])
                for half in range(2):
                    p.op("vector", lambda e, half=half, t=t: e.tensor_tensor(out=OT[0:SD, half * 4:(half + 1) * 4, t * 128:(t + 1) * 128],
                                                                            in0=c.pb[2 + half][0:SD, :].rearrange("p (a b) -> p a b", b=128),
                                                                            in1=rd[0:SD, half * 512:(half + 1) * 512].rearrange("p (a b) -> p a b", b=128), op=ALU.mult),
                         reads=[c.b_pb[2 + half], b_rd], writes=[b_OT])
            for t in range(4):
                for dg in range(4):
                    for hh in range(8):
                        p.op("tensor", lambda e, hh=hh, t=t, dg=dg: e.matmul(c.pb[4][:], lhsT=OT[0:SD, hh, t * 128:(t + 1) * 128], rhs=wo[0:SD, hh, dg * 512:(dg + 1) * 512],
                                                                             start=(hh == 0), stop=(hh == 7)),
                             reads=[b_OT, b_wo], writes=[c.b_pb[4]])
                    p.op("vector", lambda e, t=t, dg=dg: e.tensor_tensor(out=acc[:, t, dg * 512:(dg + 1) * 512], in0=acc[:, t, dg * 512:(dg + 1) * 512], in1=c.pb[4][:], op=ALU.add),
                         reads=[c.b_pb[4], b_acc[t]], writes=[b_acc[t]])
        for t in range(4):
            p.dma("sync", x_out[bi * TB + t * 128:bi * TB + (t + 1) * 128, :], acc[:, t, :], reads=[b_acc[t]], writes=[b_xout], key=f"xo{t}")

    for bi in range(ntok // TB):
        block(bi)
    return [f"xo{t}" for t in range(4)]


_PROGS = {}


def _din(nc, n, s, dt=F32):
    return nc.dram_tensor(n, list(s), dt, kind="ExternalInput").ap()


def _build_ffn():
    nc = bass.Bass("TRN2", target_bir_lowering=False)
    x_in = _din(nc, "x_in", [NTOK, D]); halo = _din(nc, "halo", [2, D]); w_in = _din(nc, "w_in", [D, 2 * DFF])
    conv_w = _din(nc, "conv_w", [3, DFF]); conv_b = _din(nc, "conv_b", [DFF]); w_out = _din(nc, "w_out", [DFF, D]); gain = _din(nc, "gain", [D])
    x_out = nc.dram_tensor("x_out", [NTOK, D], F32, kind="ExternalOutput").ap()
    p = Prog(nc); c = Ctx(); setup_common(p, c)
    keys = emit_ffn(p, c, x_in, halo, x_out, w_in, conv_w, conv_b, w_out, gain, "f")
    p.finalize(final_waits=keys)
    return nc


def _build_pool():
    nc = bass.Bass("TRN2", target_bir_lowering=False)
    x_ext = _din(nc, "x_ext", [NTOK + 2 * PH, D]); valid = _din(nc, "valid", [NTOK + 2 * PH]); pool_w = _din(nc, "pool_w", [4, 512, 512])
    pool_scale = _din(nc, "pool_scale", [D]); gain = _din(nc, "gain", [D])
    x_out = nc.dram_tensor("x_out", [NTOK, D], F32, kind="ExternalOutput").ap()
    p = Prog(nc); c = Ctx(); setup_common(p, c)
    keys = emit_pool(p, c, x_ext, valid, x_out, pool_w, pool_scale, gain, NTOK)
    p.finalize(final_waits=keys)
    return nc


def _build_swa():
    nc = bass.Bass("TRN2", target_bir_lowering=False)
    x_ext = _din(nc, "x_ext", [NTOK + 2 * SW, D]); kvalid = _din(nc, "kvalid", [NTOK + 2 * SW]); pos_ext = _din(nc, "pos_ext", [NTOK + 2 * SW], I32)
    w_qkv = _din(nc, "w_qkv", [D, 2560]); qg = _din(nc, "qg", [SD]); kg = _din(nc, "kg", [SD]); sinks = _din(nc, "sinks", [SH])
    w_o = _din(nc, "w_o", [D, D]); gain = _din(nc, "gain", [D])
    x_out = nc.dram_tensor("x_out", [NTOK, D], F32, kind="ExternalOutput").ap()
    p = Prog(nc); c = Ctx(); setup_common(p, c)
    keys = emit_swa(p, c, x_ext, kvalid, pos_ext, x_out, w_qkv, qg, kg, sinks, w_o, gain, NTOK)
    p.finalize(final_waits=keys)
    return nc


def _build_mla():
    nc = bass.Bass("TRN2", target_bir_lowering=False)
    x_seq = _din(nc, "x_seq", [S, D]); x_own = _din(nc, "x_own", [NTOK, D]); pos_seq = _din(nc, "pos_seq", [S], I32); pos_own = _din(nc, "pos_own", [NTOK], I32)
    rope_inv = _din(nc, "rope_inv", [32]); w_down = _din(nc, "w_down", [D, 1088]); qag = _din(nc, "qag", [512]); kvag = _din(nc, "kvag", [512])
    w_uq = _din(nc, "w_uq", [512, MH * MQ]); w_ukv = _din(nc, "w_ukv", [512, MH * MKV]); qn = _din(nc, "qn", [128]); qr = _din(nc, "qr", [64])
    kn = _din(nc, "kn", [128]); kr = _din(nc, "kr", [64]); w_o = _din(nc, "w_o", [D, D]); gain = _din(nc, "gain", [D])
    x_out = nc.dram_tensor("x_out", [NTOK, D], F32, kind="ExternalOutput").ap()
    p = Prog(nc); c = Ctx(); setup_common(p, c)
    keys = emit_mla(p, c, x_seq, x_own, pos_seq, pos_own, rope_inv, x_out, w_down, qag, kvag, w_uq, w_ukv, qn, qr, kn, kr, w_o, gain, S, NTOK)
    p.finalize(final_waits=keys)
    return nc


def _prog(name, fn):
    if name not in _PROGS:
        _PROGS[name] = fn()
    return _PROGS[name]


def _ext(x, b, hf, halo):
    lo = hf * NTOK - halo
    hi = (hf + 1) * NTOK + halo
    out = np.zeros((hi - lo,) + x.shape[2:], x.dtype)
    val = np.zeros((hi - lo,), np.float32)
    a = max(lo, 0); e = min(hi, S)
    out[a - lo:e - lo] = x[b, a:e]
    val[a - lo:e - lo] = 1.0
    return out, val


def _gather(res):
    return np.stack([np.concatenate([res.results[2 * b]["x_out"], res.results[2 * b + 1]["x_out"]], 0) for b in range(NB)])


def _run_pool(x, pool_w, pool_scale, gain):
    nc = _prog("pool", _build_pool)
    maps = []
    for cid in range(8):
        b, hf = divmod(cid, 2)
        xe, va = _ext(x, b, hf, PH)
        maps.append({"x_ext": xe, "valid": va, "pool_w": pool_w, "pool_scale": pool_scale, "gain": gain})
    return _gather(run_bass_kernel_spmd(nc, maps, core_ids=list(range(8))))


def _run_ffn(x, w_in, conv_w, conv_b, w_out, gain):
    nc = _prog("ffn", _build_ffn)
    maps = []
    for cid in range(8):
        b, hf = divmod(cid, 2)
        xe, _ = _ext(x, b, hf, 1)
        maps.append({"x_in": np.ascontiguousarray(xe[1:-1]), "halo": np.stack([xe[0], xe[-1]]), "w_in": w_in, "conv_w": conv_w,
                     "conv_b": conv_b, "w_out": w_out, "gain": gain})
    return _gather(run_bass_kernel_spmd(nc, maps, core_ids=list(range(8))))


def _run_swa(x, positions, w_qkv, qg, kg, sinks, w_o, gain):
    nc = _prog("swa", _build_swa)
    maps = []
    posb = np.broadcast_to(positions[None, :], (NB, S))
    for cid in range(8):
        b, hf = divmod(cid, 2)
        xe, va = _ext(x, b, hf, SW)
        pe, _ = _ext(posb, b, hf, SW)
        maps.append({"x_ext": xe, "kvalid": va, "pos_ext": np.ascontiguousarray(pe), "w_qkv": w_qkv, "qg": qg, "kg": kg, "sinks": sinks,
                     "w_o": w_o, "gain": gain})
    return _gather(run_bass_kernel_spmd(nc, maps, core_ids=list(range(8))))


def _rope_inv():
    return np.power(np.float32(10000.0), -(np.arange(0, 64, 2, dtype=np.float32) / np.float32(64))).astype(np.float32)


def _run_mla(x, positions, w_down, qag, kvag, w_uq, w_ukv, qn, qr, kn, kr, w_o, gain):
    nc = _prog("mla", _build_mla)
    maps = []
    inv = _rope_inv()
    for cid in range(8):
        b, hf = divmod(cid, 2)
        maps.append({"x_seq": np.ascontiguousarray(x[b]), "x_own": np.ascontiguousarray(x[b, hf * NTOK:(hf + 1) * NTOK]), "pos_seq": positions,
                     "pos_own": np.ascontiguousarray(positions[hf * NTOK:(hf + 1) * NTOK]), "rope_inv": inv, "w_down": w_down, "qag": qag, "kvag": kvag,
                     "w_uq": w_uq, "w_ukv": w_ukv, "qn": qn, "qr": qr, "kn": kn, "kr": kr, "w_o": w_o, "gain": gain})
    return _gather(run_bass_kernel_spmd(nc, maps, core_ids=list(range(8))))


def kernel(x, positions, norm_mix_g, norm_ffn_g, pool_w, pool_scale, swa_w_qkv, swa_q_gain, swa_k_gain, swa_sinks, swa_w_o,
           mla_w_down, mla_q_a_gain, mla_kv_a_gain, mla_w_uq, mla_w_ukv, mla_qn_gain, mla_qr_gain, mla_kn_gain, mla_kr_gain,
           mla_w_o, ffn_w_in, ffn_conv_w, ffn_conv_b, ffn_w_out):
    f = lambda a: np.ascontiguousarray(np.asarray(a, dtype=np.float32))
    x = f(x)
    positions = np.ascontiguousarray(np.asarray(positions, dtype=np.int32))
    for i in range(4):
        kind = i % 3
        j = i // 3
        g = f(norm_mix_g[i])
        if kind == 0:
            x = _run_pool(x, f(pool_w[j]), f(pool_scale[j]), g)
        elif kind == 1:
            x = _run_swa(x, positions, f(swa_w_qkv[j]), f(swa_q_gain[j]), f(swa_k_gain[j]), f(swa_sinks[j]), f(swa_w_o[j]), g)
        else:
            x = _run_mla(x, positions, f(mla_w_down[j]), f(mla_q_a_gain[j]), f(mla_kv_a_gain[j]), f(mla_w_uq[j]), f(mla_w_ukv[j]),
                         f(mla_qn_gain[j]), f(mla_qr_gain[j]), f(mla_kn_gain[j]), f(mla_kr_gain[j]), f(mla_w_o[j]), g)
        x = _run_ffn(x, f(ffn_w_in[i]), f(ffn_conv_w[i]), f(ffn_conv_b[i]), f(ffn_w_out[i]), f(norm_ffn_g[i]))
    return x
```
